# Optimizing a Trainium2 kernel written in Bass

```python
import jax, jax.numpy as jnp
from jax import lax
import numpy as np

D_MODEL = 1024
BATCH = 2
SEQ = 8192
DEPTH = 2

N_MIXERS = 2
S5_GROUP = 16
S5_GROUPS = D_MODEL // S5_GROUP
S5_STATE = 64
S5_CHUNK = 128
DT_MIN = 1e-3
DT_MAX = 1e-1
FOX_HEADS = 16
FOX_HEAD_DIM = D_MODEL // FOX_HEADS
Q_BLOCK = 128
FGATE_B_LO = 2.0
FGATE_B_HI = 6.0
D_FF = 2816
CONV_W = 3
EPS = 1e-6
N_S5 = (DEPTH + 1) // 2
N_FOX = DEPTH // 2

kernel_name = "hybrid_s5_fox_convffn_adaln"


def rmsnorm(x, g):
    x32 = x.astype(jnp.float32)
    y = x32 * lax.rsqrt(jnp.mean(x32 * x32, axis=-1, keepdims=True) + EPS)
    return (y * g.astype(jnp.float32)).astype(x.dtype)


def adaln_params(c, w, b):
    mod = jax.nn.silu(c) @ w + b
    shift, scale, gate = jnp.split(mod, 3, axis=-1)
    return shift[:, None, :], scale[:, None, :], gate[:, None, :]


def modulate(x, g, shift, scale):
    return rmsnorm(x, g) * (1.0 + scale) + shift


def s5_mixer(h, w_in, lam_re, lam_im, log_dt, b_re, b_im, c_re, c_im, d_skip, w_glu, w_out):
    f32 = jnp.float32
    Bsz, S, _ = h.shape
    u = h @ w_in
    ug = u.reshape(Bsz, S, S5_GROUPS, S5_GROUP)
    lre = lam_re.astype(f32)
    lim = lam_im.astype(f32)
    dt = jnp.exp(log_dt.astype(f32))[:, None]
    mag = jnp.exp(lre * dt)
    lb_re = mag * jnp.cos(lim * dt)
    lb_im = mag * jnp.sin(lim * dt)
    num_re = lb_re - 1.0
    den = lre * lre + lim * lim
    k_re = (num_re * lre + lb_im * lim) / den
    k_im = (lb_im * lre - num_re * lim) / den
    br = b_re.astype(f32)
    bi = b_im.astype(f32)
    bb_re = k_re[..., None] * br - k_im[..., None] * bi
    bb_im = k_re[..., None] * bi + k_im[..., None] * br
    cr = c_re.astype(f32)
    ci = c_im.astype(f32)

    def combine(left, right):
        a1r, a1i, b1r, b1i = left
        a2r, a2i, b2r, b2i = right
        ar = a2r * a1r - a2i * a1i
        ai = a2r * a1i + a2i * a1r
        b_r = a2r * b1r - a2i * b1i + b2r
        b_i = a2r * b1i + a2i * b1r + b2i
        return ar, ai, b_r, b_i

    def chunk_step(carry, u_c):
        h_re, h_im = carry
        bu_re = jnp.einsum('blgc,gpc->blgp', u_c, bb_re)
        bu_im = jnp.einsum('blgc,gpc->blgp', u_c, bb_im)
        bu_re = bu_re.at[:, 0].add(lb_re * h_re - lb_im * h_im)
        bu_im = bu_im.at[:, 0].add(lb_re * h_im + lb_im * h_re)
        a_re = jnp.broadcast_to(lb_re, bu_re.shape)
        a_im = jnp.broadcast_to(lb_im, bu_im.shape)
        _, _, hs_re, hs_im = lax.associative_scan(combine, (a_re, a_im, bu_re, bu_im), axis=1)
        y = jnp.einsum('blgp,gcp->blgc', hs_re, cr) - jnp.einsum('blgp,gcp->blgc', hs_im, ci)
        return (hs_re[:, -1], hs_im[:, -1]), y

    n_chunks = S // S5_CHUNK
    u_chunks = ug.reshape(Bsz, n_chunks, S5_CHUNK, S5_GROUPS, S5_GROUP).transpose(1, 0, 2, 3, 4)
    h0 = (jnp.zeros((Bsz, S5_GROUPS, S5_STATE), f32), jnp.zeros((Bsz, S5_GROUPS, S5_STATE), f32))
    _, ys = lax.scan(chunk_step, h0, u_chunks)
    y = ys.transpose(1, 0, 2, 3, 4).reshape(Bsz, S, D_MODEL)
    y = y + d_skip.astype(f32) * u.astype(f32)
    y = jax.nn.gelu(y)
    y = y * jax.nn.sigmoid(y @ w_glu.astype(f32))
    return (y @ w_out.astype(f32)).astype(h.dtype)


def fox_mixer(h, w_in, b_f, w_out):
    f32 = jnp.float32
    Bsz, S, _ = h.shape
    proj = h @ w_in
    q = proj[..., :D_MODEL].reshape(Bsz, S, FOX_HEADS, FOX_HEAD_DIM) * (FOX_HEAD_DIM ** -0.5)
    k = proj[..., D_MODEL:2 * D_MODEL].reshape(Bsz, S, FOX_HEADS, FOX_HEAD_DIM)
    v = proj[..., 2 * D_MODEL:3 * D_MODEL].reshape(Bsz, S, FOX_HEADS, FOX_HEAD_DIM)
    f_logit = proj[..., 3 * D_MODEL:]
    log_f = jax.nn.log_sigmoid((f_logit + b_f).astype(f32))
    F = lax.cumsum(log_f, axis=1).transpose(0, 2, 1)
    n_q = S // Q_BLOCK
    q_blocks = q.reshape(Bsz, n_q, Q_BLOCK, FOX_HEADS, FOX_HEAD_DIM).transpose(1, 0, 3, 2, 4)
    F_blocks = F.reshape(Bsz, FOX_HEADS, n_q, Q_BLOCK).transpose(2, 0, 1, 3)
    k_pos = jnp.arange(S)

    def attend_block(args):
        qi, q_blk, Fq = args
        s = jnp.einsum('bhqd,bshd->bhqs', q_blk, k).astype(f32)
        s = s + Fq[..., None] - F[:, :, None, :]
        q_pos = qi * Q_BLOCK + jnp.arange(Q_BLOCK)
        causal = k_pos[None, :] <= q_pos[:, None]
        s = jnp.where(causal, s, -jnp.inf)
        p = jax.nn.softmax(s, axis=-1)
        return jnp.einsum('bhqs,bshd->bqhd', p.astype(v.dtype), v)

    o = lax.map(attend_block, (jnp.arange(n_q), q_blocks, F_blocks))
    o = o.transpose(1, 0, 2, 3, 4).reshape(Bsz, S, D_MODEL)
    return o @ w_out


def conv_ffn(h, w_up, conv_w, conv_b, w_down):
    up = h @ w_up
    a, b = jnp.split(up, 2, axis=-1)
    S = a.shape[1]
    ap = jnp.pad(a, ((0, 0), (CONV_W - 1, 0), (0, 0)))
    a_conv = conv_b
    for i in range(CONV_W):
        a_conv = a_conv + ap[:, i:i + S] * conv_w[i]
    return (jax.nn.silu(a_conv) * b) @ w_down


def setup_inputs(seed: int = 0) -> dict:
    key = jax.random.key(seed)
    ks = jax.random.split(key, 32)
    f32 = jnp.float32
    D, G, P, Cg, H, F = D_MODEL, S5_GROUPS, S5_STATE, S5_GROUP, FOX_HEADS, D_FF
    nrm = lambda k, shape, s: jax.random.normal(k, shape, f32) * s
    x = nrm(ks[0], (BATCH, SEQ, D), 1.0)
    c = nrm(ks[1], (BATCH, D), 1.0)
    norm_g = 1.0 + nrm(ks[2], (DEPTH, 2, D), 0.01)
    ada_w = nrm(ks[3], (DEPTH, 2, D, 3 * D), 0.5 * D ** -0.5)
    ada_b = nrm(ks[4], (DEPTH, 2, 3 * D), 0.01)
    s5_w_in = nrm(ks[5], (N_S5, D, D), D ** -0.5)
    s5_lam_re = -0.5 + nrm(ks[6], (N_S5, G, P), 0.01)
    s5_lam_im = jnp.pi * jnp.arange(P, dtype=f32) + nrm(ks[7], (N_S5, G, P), 0.01)
    s5_log_dt = jax.random.uniform(ks[8], (N_S5, G), f32, np.log(DT_MIN), np.log(DT_MAX))
    s5_b_re = nrm(ks[9], (N_S5, G, P, Cg), (2.0 * Cg) ** -0.5)
    s5_b_im = nrm(ks[10], (N_S5, G, P, Cg), (2.0 * Cg) ** -0.5)
    s5_c_re = nrm(ks[11], (N_S5, G, Cg, P), (2.0 * P) ** -0.5)
    s5_c_im = nrm(ks[12], (N_S5, G, Cg, P), (2.0 * P) ** -0.5)
    s5_d = nrm(ks[13], (N_S5, D), 1.0)
    s5_w_glu = nrm(ks[14], (N_S5, D, D), D ** -0.5)
    s5_w_out = nrm(ks[15], (N_S5, D, D), D ** -0.5)
    fox_w_in = nrm(ks[16], (N_FOX, D, 3 * D + H), D ** -0.5)
    fox_b_f = jax.random.uniform(ks[17], (N_FOX, H), f32, FGATE_B_LO, FGATE_B_HI)
    fox_w_out = nrm(ks[18], (N_FOX, D, D), D ** -0.5)
    ffn_w_up = nrm(ks[19], (DEPTH, D, 2 * F), D ** -0.5)
    ffn_conv_w = nrm(ks[20], (DEPTH, CONV_W, F), CONV_W ** -0.5)
    ffn_conv_b = nrm(ks[21], (DEPTH, F), 0.01)
    ffn_w_down = nrm(ks[22], (DEPTH, F, D), F ** -0.5)
    final_g = 1.0 + nrm(ks[23], (D,), 0.01)
    return {"x": x, "c": c, "norm_g": norm_g, "ada_w": ada_w, "ada_b": ada_b,
            "s5_w_in": s5_w_in, "s5_lam_re": s5_lam_re, "s5_lam_im": s5_lam_im,
            "s5_log_dt": s5_log_dt, "s5_b_re": s5_b_re, "s5_b_im": s5_b_im,
            "s5_c_re": s5_c_re, "s5_c_im": s5_c_im, "s5_d": s5_d,
            "s5_w_glu": s5_w_glu, "s5_w_out": s5_w_out,
            "fox_w_in": fox_w_in, "fox_b_f": fox_b_f, "fox_w_out": fox_w_out,
            "ffn_w_up": ffn_w_up, "ffn_conv_w": ffn_conv_w, "ffn_conv_b": ffn_conv_b,
            "ffn_w_down": ffn_w_down, "final_g": final_g}


def reference(x, c, norm_g, ada_w, ada_b,
              s5_w_in, s5_lam_re, s5_lam_im, s5_log_dt, s5_b_re, s5_b_im,
              s5_c_re, s5_c_im, s5_d, s5_w_glu, s5_w_out,
              fox_w_in, fox_b_f, fox_w_out,
              ffn_w_up, ffn_conv_w, ffn_conv_b, ffn_w_down, final_g):
    h = x
    for i in range(DEPTH):
        j = i // N_MIXERS
        shift, scale, gate = adaln_params(c, ada_w[i, 0], ada_b[i, 0])
        hn = modulate(h, norm_g[i, 0], shift, scale)
        if i % N_MIXERS == 0:
            m = s5_mixer(hn, s5_w_in[j], s5_lam_re[j], s5_lam_im[j], s5_log_dt[j],
                         s5_b_re[j], s5_b_im[j], s5_c_re[j], s5_c_im[j], s5_d[j],
                         s5_w_glu[j], s5_w_out[j])
        else:
            m = fox_mixer(hn, fox_w_in[j], fox_b_f[j], fox_w_out[j])
        h = (h + gate * m).astype(x.dtype)
        shift, scale, gate = adaln_params(c, ada_w[i, 1], ada_b[i, 1])
        hn = modulate(h, norm_g[i, 1], shift, scale)
        f = conv_ffn(hn, ffn_w_up[i], ffn_conv_w[i], ffn_conv_b[i], ffn_w_down[i])
        h = (h + gate * f).astype(x.dtype)
    return rmsnorm(h, final_g)
```

```python
import contextlib
import numpy as np
import concourse.bass as bass
import concourse.mybir as mybir
from concourse.bass_utils import run_bass_kernel_spmd

F32 = mybir.dt.float32
BF16 = mybir.dt.bfloat16
AF = mybir.ActivationFunctionType
ALU = mybir.AluOpType

D = 1024
B = 2
S = 8192
NCORE = 8
TPC = 2048
DFF = 2816
H = 16
DH = 64
EPS = 1e-6
ENGS = ("pe", "act", "dve", "pool", "sp")
NDMASEM = 40


class Prog:
    def __init__(self, nc):
        self.nc = nc
        self.streams = {e: [] for e in ENGS}
        self.count = {e: 0 for e in ENGS}
        self.seen = {e: {} for e in ENGS}
        self.last_w = {}
        self.readers = {}
        self.dma_sems = []
        self.ndma = 0

    def _need(self, eng, tok, waits):
        if tok is None:
            return
        if tok[0] == "eng":
            key, val = ("eng", tok[1]), tok[2] + 1
        else:
            key, val = ("dma", tok[1]), tok[2]
        if self.seen[eng].get(key, 0) >= val:
            return
        waits[key] = max(waits.get(key, 0), val)

    def _deps(self, eng, reads, writes):
        waits = {}
        for k in list(reads) + list(writes):
            self._need(eng, self.last_w.get(k), waits)
        for k in writes:
            for t in self.readers.get(k, ()):
                self._need(eng, t, waits)
        return waits

    def _commit(self, tok, reads, writes):
        for k in writes:
            self.last_w[k] = tok
            self.readers[k] = []
        for k in reads:
            lst = self.readers.setdefault(k, [])
            lst[:] = [t for t in lst if not (t[0] == tok[0] and t[1] == tok[1])]
            lst.append(tok)

    def op(self, eng, fn, reads=(), writes=()):
        waits = self._deps(eng, reads, writes)
        for key, val in waits.items():
            self.seen[eng][key] = val
        idx = self.count[eng]
        self.count[eng] += 1
        tok = ("eng", eng, idx)
        self.streams[eng].append(("op", fn, waits))
        self._commit(tok, reads, writes)
        return tok

    def dma(self, eng, fn, reads=(), writes=()):
        waits = self._deps(eng, reads, writes)
        si = self.ndma % NDMASEM
        self.ndma += 1
        while len(self.dma_sems) <= si:
            self.dma_sems.append(0)
        self.dma_sems[si] += 16
        val = self.dma_sems[si]
        if val > 16 and self.seen[eng].get(("dma", si), 0) < val - 16:
            waits[("dma", si)] = max(waits.get(("dma", si), 0), val - 16)
        for key, v in waits.items():
            self.seen[eng][key] = max(self.seen[eng].get(key, 0), v)
        tok = ("dma", si, val)
        self.streams[eng].append(("dma", fn, waits, si))
        self._commit(tok, reads, writes)
        return tok

    def emit(self):
        nc = self.nc
        with contextlib.ExitStack() as st:
            esem = {e: st.enter_context(nc.semaphore("p_" + e)) for e in ENGS}
            dsem = [st.enter_context(nc.semaphore("d%d" % i)) for i in range(len(self.dma_sems))]
            block = st.enter_context(nc.Block())
            fw = {}
            for e in ENGS:
                if self.count[e]:
                    fw[("eng", e)] = self.count[e]
            for i, v in enumerate(self.dma_sems):
                fw[("dma", i)] = v
            self.streams["sp"].append(("final", None, fw))

            def run(ename, eobj):
                for item in self.streams[ename]:
                    for key, val in item[2].items():
                        s = esem[key[1]] if key[0] == "eng" else dsem[key[1]]
                        eobj.wait_ge(s, val)
                    if item[0] == "op":
                        item[1](eobj).then_inc(esem[ename], 1)
                    elif item[0] == "dma":
                        item[1](eobj).then_inc(dsem[item[3]], 16)

            block.tensor(lambda e: run("pe", e))
            block.scalar(lambda e: run("act", e))
            block.vector(lambda e: run("dve", e))
            block.gpsimd(lambda e: run("pool", e))
            block.sync(lambda e: run("sp", e))


class KB:
    def __init__(self):
        self.nc = bass.Bass("TRN2", target_bir_lowering=False)
        self.st = contextlib.ExitStack()
        self.P = Prog(self.nc)
        self.outs = []
        self._rr = 0

    def din(self, name, shape, dt=F32):
        return self.nc.dram_tensor(name, list(shape), dt, kind="ExternalInput").ap()

    def dout(self, name, shape, dt=F32):
        self.outs.append(name)
        return self.nc.dram_tensor(name, list(shape), dt, kind="ExternalOutput").ap()

    def sb(self, name, shape, dt=F32):
        return self.st.enter_context(self.nc.sbuf_tensor("sb_" + name, list(shape), dt))

    def ps(self, name, shape=(128, 512), dt=F32):
        return self.st.enter_context(self.nc.psum_tensor("pm_" + name, list(shape), dt))

    def dma(self, eng, out, in_, reads=(), writes=()):
        return self.P.dma(eng, lambda e: e.dma_start(out=out, in_=in_), reads, writes)

    def mm(self, out, lhsT, rhs, start, stop, reads=(), writes=()):
        return self.P.op("pe", lambda e: e.matmul(out, lhsT=lhsT, rhs=rhs, start=start, stop=stop), reads, writes)

    def tr(self, out, in_, ident, reads=(), writes=()):
        return self.P.op("pe", lambda e: e.transpose(out, in_, ident), reads, writes)

    def act(self, out, in_, func, bias=None, scale=None, reads=(), writes=()):
        kw = {}
        if bias is not None:
            kw["bias"] = bias
        if scale is not None:
            kw["scale"] = scale
        return self.P.op("act", lambda e: e.activation(out=out, in_=in_, func=func, **kw), reads, writes)

    def tt(self, eng, out, in0, in1, op, reads=(), writes=()):
        return self.P.op(eng, lambda e: e.tensor_tensor(out=out, in0=in0, in1=in1, op=op), reads, writes)

    def ts(self, eng, out, in0, s1, op0, s2=None, op1=None, reads=(), writes=()):
        if op1 is None:
            return self.P.op(eng, lambda e: e.tensor_scalar(out=out, in0=in0, scalar1=s1, scalar2=None, op0=op0), reads, writes)
        return self.P.op(eng, lambda e: e.tensor_scalar(out=out, in0=in0, scalar1=s1, scalar2=s2, op0=op0, op1=op1), reads, writes)

    def stt(self, eng, out, in0, scalar, in1, op0, op1, reads=(), writes=()):
        return self.P.op(eng, lambda e: e.scalar_tensor_tensor(out=out, in0=in0, scalar=scalar, in1=in1, op0=op0, op1=op1), reads, writes)

    def copy(self, eng, out, in_, reads=(), writes=()):
        if eng == "act":
            return self.P.op("act", lambda e: e.copy(out=out, in_=in_), reads, writes)
        return self.P.op(eng, lambda e: e.tensor_copy(out=out, in_=in_), reads, writes)

    def memset(self, eng, ap, val, writes=()):
        return self.P.op(eng, lambda e: e.memset(ap, val), (), writes)

    def evac_eng(self):
        self._rr += 1
        return "act" if self._rr % 2 else "dve"

    def finish(self):
        self.P.emit()
        self.st.close()
        return self.nc


def run_spmd(kb, in_maps):
    nc = kb.finish()
    res = run_bass_kernel_spmd(nc, in_maps, core_ids=list(range(len(in_maps))))
    return res.results


def load_vec_tiles(kb, name, dram_ap, ntile, eng="sp"):
    t = kb.sb(name, [128, ntile])
    kb.dma(eng, t[:], dram_ap, writes=[name])
    return t


def emit_rmsnorm_mod(kb, pref, h, hkey, gs, shift, hn, hnkey, ones_bf, nblk, scratch, ps_pool, NB=512, KT=8):
    sq, rstd, tmp = scratch["sq"], scratch["rstd"], scratch["tmp"]
    for nb in range(nblk):
        c0, c1 = nb * NB, (nb + 1) * NB
        ps = ps_pool[nb % len(ps_pool)]
        psk = "nps%d" % (nb % len(ps_pool))
        for kt in range(KT):
            sqk = "nsq%d" % (kt % 2)
            kb.act(sq[:, kt % 2, :], h[:, kt, c0:c1], AF.Square, reads=[hkey(kt, nb)], writes=[sqk])
            kb.mm(ps[:, :], ones_bf[:, :], sq[:, kt % 2, :], kt == 0, kt == KT - 1, reads=[sqk, "ones"], writes=[psk])
        kb.act(rstd[:, :], ps[:, :], AF.Ln, bias=scratch["eps"][:, 0:1], reads=[psk, "neps"], writes=["nrstd"])
        kb.act(rstd[:, :], rstd[:, :], AF.Exp, scale=-0.5, reads=["nrstd"], writes=["nrstd"])
        for kt in range(KT):
            if shift is None:
                kb.stt("dve", hn[:, kt, c0:c1], h[:, kt, c0:c1], gs[:, kt:kt + 1], rstd[:, :], ALU.mult, ALU.mult,
                       reads=[hkey(kt, nb), "nrstd", "gs_" + pref], writes=[hnkey(kt, nb)])
            else:
                tk = "ntmp%d" % (kt % 2)
                kb.stt("dve", tmp[:, kt % 2, :], h[:, kt, c0:c1], gs[:, kt:kt + 1], rstd[:, :], ALU.mult, ALU.mult,
                       reads=[hkey(kt, nb), "nrstd", "gs_" + pref], writes=[tk])
                kb.act(hn[:, kt, c0:c1], tmp[:, kt % 2, :], AF.Identity, bias=shift[:, kt:kt + 1],
                       reads=[tk, "shift_" + pref], writes=[hnkey(kt, nb)])


def norm_scratch(kb, NB=512):
    eps = kb.sb("n_eps", [128, 1])
    kb.memset("pool", eps[:], EPS, writes=["neps"])
    return {"sq": kb.sb("n_sq", [128, 2, NB], BF16), "rstd": kb.sb("n_rstd", [128, NB]),
            "tmp": kb.sb("n_tmp", [128, 2, NB]), "eps": eps}


def emit_gs(kb, pref, g_t, scale_t, gs_t, KT=8):
    kb.stt("dve", gs_t[:, :], scale_t[:, :], 1.0, g_t[:, :], ALU.add, ALU.mult,
           reads=["g_" + pref, "scale_" + pref], writes=["gs_" + pref])


def emit_linear(kb, w, wkey, x, xkey, nblk, KT, MT, sink, ps_pool, pskey, NB=512):
    i = 0
    for nb in range(nblk):
        c0, c1 = nb * NB, (nb + 1) * NB
        for mt in range(MT):
            ps = ps_pool[i % len(ps_pool)]
            pk = "%s%d" % (pskey, i % len(ps_pool))
            i += 1
            for kt in range(KT):
                kb.mm(ps[:, :], w[:, kt, mt * 128:(mt + 1) * 128], x[:, kt, c0:c1], kt == 0, kt == KT - 1,
                      reads=[wkey, xkey(kt, nb)], writes=[pk])
            sink(mt, nb, ps, pk)


def vec_tiles(v):
    v = np.asarray(v, np.float32)
    return np.ascontiguousarray(v.reshape(-1, 128).T)


def tok_shard_T(a):
    out = []
    for b in range(B):
        for q in range(4):
            out.append(np.ascontiguousarray(a[b, q * TPC:(q + 1) * TPC, :].T))
    return out


def tok_unshard_T(lst, C):
    out = np.empty((B, S, C), np.float32)
    for b in range(B):
        for q in range(4):
            out[b, q * TPC:(q + 1) * TPC, :] = lst[b * 4 + q].T
    return out


def build_k0():
    kb = KB()
    cT = kb.din("cT", [128, 8, 2])
    aw = kb.din("aw", [128, 4, 8, 384])
    ab = kb.din("ab", [128, 4, 3])
    mod = kb.dout("mod", [128, 4, 3, 2])
    c_sb = kb.sb("c_sb", [128, 8, 2])
    sc = kb.sb("sc", [128, 8, 2])
    w_sb = kb.sb("w_sb", [128, 4, 8, 384])
    b_sb = kb.sb("b_sb", [128, 4, 3])
    o_sb = kb.sb("o_sb", [128, 4, 3, 2])
    pss = [kb.ps("ps%d" % i, [128, 2]) for i in range(4)]
    kb.dma("sp", c_sb[:], cT, writes=["c"])
    kb.dma("act", b_sb[:], ab, writes=["b"])
    for l in range(4):
        kb.dma(("sp", "act", "pool", "sp")[l], w_sb[:, l], aw[:, l], writes=["w%d" % l])
    kb.act(sc[:], c_sb[:], AF.Silu, reads=["c"], writes=["sc"])
    i = 0
    for l in range(4):
        for mt in range(3):
            ps = pss[i % 4]
            pk = "ps%d" % (i % 4)
            i += 1
            for kt in range(8):
                kb.mm(ps[:, :], w_sb[:, l, kt, mt * 128:(mt + 1) * 128], sc[:, kt, :], kt == 0, kt == 7,
                      reads=["w%d" % l, "sc"], writes=[pk])
            kb.ts("dve", o_sb[:, l, mt, :], ps[:, :], b_sb[:, l, mt:mt + 1], ALU.add, reads=[pk, "b"], writes=["o"])
    kb.dma("sp", mod, o_sb[:], reads=["o"])
    return kb


def run_k0(c, ada_w, ada_b):
    cT = np.ascontiguousarray(c.T.reshape(8, 128, 2).transpose(1, 0, 2))
    aw4 = ada_w.reshape(4, 8, 128, 3072)
    ab4 = ada_b.reshape(4, 3072)
    maps = []
    for core in range(NCORE):
        cols = slice(core * 384, (core + 1) * 384)
        aw = np.ascontiguousarray(aw4[:, :, :, cols].transpose(2, 0, 1, 3))
        ab = np.ascontiguousarray(ab4[:, cols].reshape(4, 3, 128).transpose(2, 0, 1))
        maps.append({"cT": cT, "aw": aw, "ab": ab})
    res = run_spmd(build_k0(), maps)
    full = np.empty((4, 3072, 2), np.float32)
    for core in range(NCORE):
        m = res[core]["mod"]
        full[:, core * 384:(core + 1) * 384, :] = m.transpose(1, 2, 0, 3).reshape(4, 384, 2)
    return full


def mod_tiles(full, l, b):
    v = full[l, :, b]
    return vec_tiles(v[0:1024]), vec_tiles(v[1024:2048]), vec_tiles(v[2048:3072])


def build_normproj(M):
    MT = (M + 127) // 128
    kb = KB()
    hT = kb.din("hT", [1024, TPC])
    g_d = kb.din("g", [128, 8])
    sh_d = kb.din("shift", [128, 8])
    sc_d = kb.din("scale", [128, 8])
    w_d = kb.din("w", [1024, M])
    out = kb.dout("out", [MT * 128, TPC])
    h = kb.sb("h", [128, 8, TPC])
    hn = kb.sb("hn", [128, 8, TPC], BF16)
    w = kb.sb("w_sb", [128, 8, MT * 128], BF16)
    ones = kb.sb("ones", [128, 128], BF16)
    g_t = kb.sb("g_t", [128, 8]); sh_t = kb.sb("sh_t", [128, 8]); sc_t = kb.sb("sc_t", [128, 8]); gs_t = kb.sb("gs_t", [128, 8])
    ob = kb.sb("ob", [128, 4, 512])
    scr = norm_scratch(kb)
    nps = [kb.ps("nps%d" % i) for i in range(2)]
    lps = [kb.ps("lps%d" % i) for i in range(4)]
    kb.memset("pool", ones[:], 1.0 / D, writes=["ones"])
    if MT * 128 != M:
        kb.memset("pool", w[:], 0.0, writes=["w"])
    hv = hT.rearrange("(kt p) t -> p kt t", p=128)
    for kt in range(8):
        kb.dma(("sp", "act")[kt % 2], h[:, kt, :], hv[:, kt, :], writes=["h%d" % kt])
    kb.dma("sp", g_t[:], g_d, writes=["g_n"]); kb.dma("sp", sh_t[:], sh_d, writes=["shift_n"]); kb.dma("sp", sc_t[:], sc_d, writes=["scale_n"])
    wv = w_d.rearrange("(kt p) m -> p kt m", p=128)
    for kt in range(8):
        kb.dma("pool", w[:, kt, 0:M], wv[:, kt, :], writes=["w"])
    emit_gs(kb, "n", g_t, sc_t, gs_t)
    emit_rmsnorm_mod(kb, "n", h, lambda kt, nb: "h%d" % kt, gs_t, sh_t, hn, lambda kt, nb: "hn%d_%d" % (kt, nb), ones, 4, scr, nps)

    cnt = [0]

    def sink(mt, nb, ps, pk):
        j = cnt[0] % 4
        cnt[0] += 1
        ok = "ob%d" % j
        kb.copy(kb.evac_eng(), ob[:, j, :], ps[:, :], reads=[pk], writes=[ok])
        kb.dma(("sp", "pool")[j % 2], out[mt * 128:(mt + 1) * 128, nb * 512:(nb + 1) * 512], ob[:, j, :], reads=[ok])

    emit_linear(kb, w, "w", hn, lambda kt, nb: "hn%d_%d" % (kt, nb), 4, 8, MT, sink, lps, "lps")
    return kb


def run_normproj(hT_list, g, mods, l, w):
    M = w.shape[1]
    kb = build_normproj(M)
    maps = []
    for core in range(NCORE):
        sh, sc, _ = mod_tiles(mods, l, core // 4)
        maps.append({"hT": hT_list[core], "g": vec_tiles(g), "shift": sh, "scale": sc, "w": np.ascontiguousarray(w)})
    res = run_spmd(kb, maps)
    return [r["out"] for r in res]


NCH = S // 8
PI = float(np.pi)


def s5_consts():
    esel = np.zeros((128, 8, 240), np.float32)
    for g in range(8):
        for c in range(16):
            esel[16 * g + c, g, 112 + c] = 1.0
    mask = np.zeros((128, 128), np.float32)
    for m in range(8):
        for t in range(8):
            if m + t >= 7:
                mask[16 * m:16 * m + 16, 16 * t:16 * t + 16] = 1.0
    return esel, mask, np.eye(128, dtype=np.float32)


def build_k2():
    kb = KB()
    u_d = kb.din("u", [256, S])
    lre_d = kb.din("lre", [128, 8]); lim_d = kb.din("lim", [128, 8]); ldt_d = kb.din("ldt", [128, 8])
    br_d = kb.din("br", [128, 8, 16]); bi_d = kb.din("bi", [128, 8, 16])
    cr_d = kb.din("cr", [128, 8, 16]); ci_d = kb.din("ci", [128, 8, 16])
    dsk_d = kb.din("dsk", [128, 2])
    esel_d = kb.din("esel", [128, 8, 240]); mask_d = kb.din("mask", [128, 128]); id_d = kb.din("ident", [128, 128])
    y_d = kb.dout("y", [256, S])

    R = kb.sb("R", [128, 8192])
    ubf = R[:, :].bitcast(BF16).rearrange("p (c t) -> p c t", c=2)
    U = kb.sb("U", [128, 16, NCH], BF16)
    Sbf = kb.sb("Sbf", [128, 8, 2, NCH + 1], BF16)
    u32 = kb.sb("u32", [128, 2, 2048])
    yout = kb.sb("yout", [128, 2, 2048])
    esel = kb.sb("esel", [128, 8, 240], BF16)
    mask = kb.sb("mask", [128, 128]); ident = kb.sb("ident", [128, 128])
    dsk = kb.sb("dsk", [128, 2])
    small = {}

    def sm(name, n=8):
        small[name] = kb.sb("s_" + name, [128, n])
        return small[name]

    lre = sm("lre"); lim = sm("lim"); ldt = sm("ldt")
    dt = sm("dt"); aa = sm("aa"); th = sm("th"); mag = sm("mag"); sn = sm("sn"); cs = sm("cs")
    t1 = sm("t1"); t2 = sm("t2"); t3 = sm("t3"); t4 = sm("t4")
    ki = kb.sb("s_ki", [128, 8], mybir.dt.int32)
    lbr = sm("lbr"); lbi = sm("lbi"); kre = sm("kre"); kim = sm("kim"); den = sm("den")
    ivr = sm("ivr"); ivi = sm("ivi")
    Lre = kb.sb("Lre", [128, 8, 16]); Lim = kb.sb("Lim", [128, 8, 16])
    Pre = kb.sb("Pre", [128, 8, 10]); Pim = kb.sb("Pim", [128, 8, 10]); nPim = kb.sb("nPim", [128, 8, 10])
    br = kb.sb("br", [128, 8, 16]); bi = kb.sb("bi", [128, 8, 16]); cr = kb.sb("cr", [128, 8, 16]); ci = kb.sb("ci", [128, 8, 16])
    bbr = kb.sb("bbr", [128, 8, 16]); bbi = kb.sb("bbi", [128, 8, 16]); tb = kb.sb("tb", [128, 8, 16]); tb2 = kb.sb("tb2", [128, 8, 16])
    BLr = kb.sb("BLr", [128, 8, 128]); BLi = kb.sb("BLi", [128, 8, 128])
    CLr = kb.sb("CLr", [128, 8, 256]); CLni = kb.sb("CLni", [128, 8, 256])
    big1 = kb.sb("big1", [128, 8, 256]); big2 = kb.sb("big2", [128, 8, 256])
    Mtab = kb.sb("Mtab", [128, 16, 128], BF16)
    WTr = kb.sb("WTr", [128, 16, 64], BF16); WTi = kb.sb("WTi", [128, 16, 64], BF16)
    CLbr = kb.sb("CLbr", [128, 8, 128], BF16); CLbni = kb.sb("CLbni", [128, 8, 128], BF16)
    gz = kb.sb("gz", [128, 2, 256]); gw = kb.sb("gw", [128, 2, 256]); gs_ = kb.sb("gs", [128, 2, 256])
    pss = [kb.ps("ps%d" % i) for i in range(8)]

    kb.dma("sp", lre[:], lre_d, writes=["lre"]); kb.dma("sp", lim[:], lim_d, writes=["lim"]); kb.dma("sp", ldt[:], ldt_d, writes=["ldt"])
    kb.dma("act", br[:], br_d, writes=["br"]); kb.dma("act", bi[:], bi_d, writes=["bi"])
    kb.dma("act", cr[:], cr_d, writes=["cr"]); kb.dma("act", ci[:], ci_d, writes=["ci"])
    kb.dma("sp", dsk[:], dsk_d, writes=["dsk"]); kb.dma("sp", mask[:], mask_d, writes=["mask"]); kb.dma("sp", ident[:], id_d, writes=["ident"])
    kb.dma("pool", esel[:], esel_d, writes=["esel"])
    uv = u_d.rearrange("(c p) t -> p c t", p=128)
    for c in range(2):
        for hf in range(2):
            kb.dma("pool", ubf[:, c, hf * 4096:(hf + 1) * 4096], uv[:, c, hf * 4096:(hf + 1) * 4096], writes=["ubf%d" % c])

    E = "dve"

    def tt(o, a, b, op, r, w):
        kb.tt(E, o, a, b, op, reads=r, writes=w)

    kb.act(dt[:], ldt[:], AF.Exp, reads=["ldt"], writes=["dt"])
    tt(aa[:], lre[:], dt[:], ALU.mult, ["lre", "dt"], ["aa"])
    tt(th[:], lim[:], dt[:], ALU.mult, ["lim", "dt"], ["th"])
    kb.act(mag[:], aa[:], AF.Exp, reads=["aa"], writes=["mag"])

    def sin_of(out, arg_key, arg, shift, okey):
        kb.ts(E, t1[:], arg, shift, ALU.add, reads=[arg_key], writes=["t1"])
        kb.ts(E, t2[:], t1[:], 1.0 / (2 * PI), ALU.mult, reads=["t1"], writes=["t2"])
        kb.copy(E, ki[:], t2[:], reads=["t2"], writes=["ki"])
        kb.copy(E, t2[:], ki[:], reads=["ki"], writes=["t2"])
        kb.stt(E, t1[:], t2[:], -2 * PI, t1[:], ALU.mult, ALU.add, reads=["t2", "t1"], writes=["t1"])
        kb.ts(E, t3[:], t1[:], PI, ALU.is_gt, reads=["t1"], writes=["t3"])
        kb.ts(E, t4[:], t1[:], -PI, ALU.is_lt, reads=["t1"], writes=["t4"])
        tt(t4[:], t4[:], t3[:], ALU.subtract, ["t3", "t4"], ["t4"])
        kb.stt(E, t1[:], t4[:], 2 * PI, t1[:], ALU.mult, ALU.add, reads=["t4", "t1"], writes=["t1"])
        kb.act(out, t1[:], AF.Sin, reads=["t1"], writes=[okey])

    sin_of(sn[:], "th", th[:], 0.0, "sn")
    sin_of(cs[:], "th", th[:], PI / 2, "cs")
    tt(lbr[:], mag[:], cs[:], ALU.mult, ["mag", "cs"], ["lbr"])
    tt(lbi[:], mag[:], sn[:], ALU.mult, ["mag", "sn"], ["lbi"])
    kb.ts(E, t1[:], lbr[:], -1.0, ALU.add, reads=["lbr"], writes=["t1"])
    tt(t2[:], lre[:], lre[:], ALU.mult, ["lre"], ["t2"])
    tt(t3[:], lim[:], lim[:], ALU.mult, ["lim"], ["t3"])
    tt(den[:], t2[:], t3[:], ALU.add, ["t2", "t3"], ["den"])
    kb.P.op(E, lambda e: e.reciprocal(out=den[:], in_=den[:]), ["den"], ["den"])
    tt(t2[:], t1[:], lre[:], ALU.mult, ["t1", "lre"], ["t2"])
    tt(t3[:], lbi[:], lim[:], ALU.mult, ["lbi", "lim"], ["t3"])
    tt(t2[:], t2[:], t3[:], ALU.add, ["t2", "t3"], ["t2"])
    tt(kre[:], t2[:], den[:], ALU.mult, ["t2", "den"], ["kre"])
    tt(t2[:], lbi[:], lre[:], ALU.mult, ["lbi", "lre"], ["t2"])
    tt(t3[:], t1[:], lim[:], ALU.mult, ["t1", "lim"], ["t3"])
    tt(t2[:], t2[:], t3[:], ALU.subtract, ["t2", "t3"], ["t2"])
    tt(kim[:], t2[:], den[:], ALU.mult, ["t2", "den"], ["kim"])
    tt(t2[:], mag[:], mag[:], ALU.mult, ["mag"], ["t2"])
    kb.P.op(E, lambda e: e.reciprocal(out=t2[:], in_=t2[:]), ["t2"], ["t2"])
    tt(ivr[:], lbr[:], t2[:], ALU.mult, ["lbr", "t2"], ["ivr"])
    kb.stt(E, ivi[:], lbi[:], -1.0, t2[:], ALU.mult, ALU.mult, reads=["lbi", "t2"], writes=["ivi"])

    def cmul(o_re, o_im, a_re, a_im, b_re, b_im, rk, wk, ta, tb_):
        tt(ta, a_re, b_re, ALU.mult, rk, ["cm_a"])
        tt(tb_, a_im, b_im, ALU.mult, rk, ["cm_b"])
        tt(o_re, ta, tb_, ALU.subtract, ["cm_a", "cm_b"], [wk + "r"])
        tt(ta, a_re, b_im, ALU.mult, rk, ["cm_a"])
        tt(tb_, a_im, b_re, ALU.mult, rk, ["cm_b"])
        tt(o_im, ta, tb_, ALU.add, ["cm_a", "cm_b"], [wk + "i"])

    kb.memset(E, Lre[:, :, 7], 1.0, writes=["L7r"]); kb.memset(E, Lim[:, :, 7], 0.0, writes=["L7i"])
    kb.copy(E, Lre[:, :, 8], lbr[:], reads=["lbr"], writes=["L8r"]); kb.copy(E, Lim[:, :, 8], lbi[:], reads=["lbi"], writes=["L8i"])
    kb.copy(E, Lre[:, :, 6], ivr[:], reads=["ivr"], writes=["L6r"]); kb.copy(E, Lim[:, :, 6], ivi[:], reads=["ivi"], writes=["L6i"])
    for n in range(9, 16):
        cmul(Lre[:, :, n], Lim[:, :, n], Lre[:, :, n - 1], Lim[:, :, n - 1], lbr[:], lbi[:],
             ["L%dr" % (n - 1), "L%di" % (n - 1), "lbr", "lbi"], "L%d" % n, t1[:], t2[:])
    for n in range(5, -1, -1):
        cmul(Lre[:, :, n], Lim[:, :, n], Lre[:, :, n + 1], Lim[:, :, n + 1], ivr[:], ivi[:],
             ["L%dr" % (n + 1), "L%di" % (n + 1), "ivr", "ivi"], "L%d" % n, t1[:], t2[:])
    lall = ["L%d%s" % (n, s) for n in range(16) for s in "ri"]
    kb.copy(E, Pre[:, :, 0], Lre[:, :, 15], reads=["L15r"], writes=["P0r"]); kb.copy(E, Pim[:, :, 0], Lim[:, :, 15], reads=["L15i"], writes=["P0i"])
    for l in range(1, 10):
        cmul(Pre[:, :, l], Pim[:, :, l], Pre[:, :, l - 1], Pim[:, :, l - 1], Pre[:, :, l - 1], Pim[:, :, l - 1],
             ["P%dr" % (l - 1), "P%di" % (l - 1)], "P%d" % l, t1[:], t2[:])
    pall = ["P%d%s" % (l, s) for l in range(10) for s in "ri"]
    kb.ts(E, nPim[:], Pim[:], -1.0, ALU.mult, reads=pall, writes=["nP"])
    pall = pall + ["nP"]
    kb3 = lambda a: a.unsqueeze(2).to_broadcast([128, 8, 16])
    tt(tb[:], br[:], kb3(kre[:]), ALU.mult, ["br", "kre"], ["tb"])
    tt(tb2[:], bi[:], kb3(kim[:]), ALU.mult, ["bi", "kim"], ["tb2"])
    tt(bbr[:], tb[:], tb2[:], ALU.subtract, ["tb", "tb2"], ["bbr"])
    tt(tb[:], bi[:], kb3(kre[:]), ALU.mult, ["bi", "kre"], ["tb"])
    tt(tb2[:], br[:], kb3(kim[:]), ALU.mult, ["br", "kim"], ["tb2"])
    tt(bbi[:], tb[:], tb2[:], ALU.add, ["tb", "tb2"], ["bbi"])

    def v4(t, n):
        return t[:, :, :].rearrange("p g (m c) -> p g m c", c=16)

    def lb4(Lt, lo, n):
        return Lt[:, :, lo:lo + n].unsqueeze(3).to_broadcast([128, 8, n, 16])

    def vb4(v, n):
        return v[:, :, :].unsqueeze(2).to_broadcast([128, 8, n, 16])

    b1 = big1[:, :, 0:128].rearrange("p g (m c) -> p g m c", c=16)
    b2 = big2[:, :, 0:128].rearrange("p g (m c) -> p g m c", c=16)
    tt(b1, lb4(Lre, 7, 8), vb4(bbr, 8), ALU.mult, lall + ["bbr"], ["big1"])
    tt(b2, lb4(Lim, 7, 8), vb4(bbi, 8), ALU.mult, lall + ["bbi"], ["big2"])
    tt(v4(BLr, 8), b1, b2, ALU.subtract, ["big1", "big2"], ["BLr"])
    tt(b1, lb4(Lre, 7, 8), vb4(bbi, 8), ALU.mult, lall + ["bbi"], ["big1"])
    tt(b2, lb4(Lim, 7, 8), vb4(bbr, 8), ALU.mult, lall + ["bbr"], ["big2"])
    tt(v4(BLi, 8), b1, b2, ALU.add, ["big1", "big2"], ["BLi"])
    c1 = v4(big1, 16); c2 = v4(big2, 16)
    tt(c1, lb4(Lre, 0, 16), vb4(cr, 16), ALU.mult, lall + ["cr"], ["big1"])
    tt(c2, lb4(Lim, 0, 16), vb4(ci, 16), ALU.mult, lall + ["ci"], ["big2"])
    tt(v4(CLr, 16), c1, c2, ALU.subtract, ["big1", "big2"], ["CLr"])
    tt(c1, lb4(Lre, 0, 16), vb4(ci, 16), ALU.mult, lall + ["ci"], ["big1"])
    tt(c2, lb4(Lim, 0, 16), vb4(cr, 16), ALU.mult, lall + ["cr"], ["big2"])
    kb.stt(E, v4(CLni, 16), c1, -1.0, c2, ALU.mult, ALU.subtract, reads=["big1", "big2"], writes=["CLni"])
    kb.copy("act", CLbr[:, :, :], CLr[:, :, 128:256], reads=["CLr"], writes=["CLbr"])
    kb.copy("act", CLbni[:, :, :], CLni[:, :, 128:256], reads=["CLni"], writes=["CLbni"])

    for gl in range(16):
        gp, par = gl // 2, gl % 2
        rows = slice(par * 64, par * 64 + 64)
        ps = pss[gl % 2]; pk = "ps%d" % (gl % 2)
        kb.mm(ps[:, 0:128], BLr[rows, gp, :], CLr[rows, gp, 0:128], True, False, reads=["BLr", "CLr"], writes=[pk])
        kb.mm(ps[:, 0:128], BLi[rows, gp, :], CLni[rows, gp, 0:128], False, True, reads=["BLi", "CLni"], writes=[pk])
        kb.tt("dve", Mtab[:, gl, :], ps[:, 0:128], mask[:, :], ALU.mult, reads=[pk, "mask"], writes=["Mtab"])
        ps2 = pss[2 + gl % 2]; pk2 = "ps%d" % (2 + gl % 2)
        kb.tr(ps2[:, 0:64], BLr[rows, gp, :], ident[rows, par * 64:par * 64 + 64], reads=["BLr", "ident"], writes=[pk2])
        kb.tr(ps2[:, 64:128], BLi[rows, gp, :], ident[rows, par * 64:par * 64 + 64], reads=["BLi", "ident"], writes=[pk2])
        kb.copy("act", WTr[:, gl, :], ps2[:, 0:64], reads=[pk2], writes=["WT"])
        kb.copy("act", WTi[:, gl, :], ps2[:, 64:128], reads=[pk2], writes=["WT"])

    for gl in range(16):
        ct, g8 = gl // 8, gl % 8
        for nb in range(2):
            ps = pss[4 + (gl * 2 + nb) % 2]; pk = "ps%d" % (4 + (gl * 2 + nb) % 2)
            for m in range(8):
                st = nb * 4096 + (7 - m)
                kb.mm(ps[:, :], esel[:, g8, 112 - 16 * m:240 - 16 * m], ubf[:, ct, st:(nb + 1) * 4096:8], m == 0, m == 7,
                      reads=["esel", "ubf%d" % ct], writes=[pk])
            kb.copy(kb.evac_eng(), U[:, gl, nb * 512:(nb + 1) * 512], ps[:, :], reads=[pk], writes=["U%d_%d" % (gl, nb)])

    kb.memset("pool", Sbf[:, :, :, 0:1], 0.0, writes=["Sbf0"])
    for gp in range(8):
        eng = "dve"
        o = (gp % 2) * 4096
        Sa = R[:, o:o + 2048].rearrange("p (r k) -> p r k", r=2)
        Sb = R[:, o + 2048:o + 4096].rearrange("p (r k) -> p r k", r=2)
        ka, kbk = "Sa%d" % (gp % 2), "Sb%d" % (gp % 2)
        for nb in range(2):
            pr = pss[(gp * 2 + nb) % 2 * 2]; pi_ = pss[(gp * 2 + nb) % 2 * 2 + 1]
            prk = "ps%d" % ((gp * 2 + nb) % 2 * 2); pik = "ps%d" % ((gp * 2 + nb) % 2 * 2 + 1)
            for par in range(2):
                gl = 2 * gp + par
                kb.mm(pr[par * 64:par * 64 + 64, :], WTr[:, gl, :], U[:, gl, nb * 512:(nb + 1) * 512], True, True,
                      reads=["WT", "U%d_%d" % (gl, nb)], writes=[prk])
                kb.mm(pi_[par * 64:par * 64 + 64, :], WTi[:, gl, :], U[:, gl, nb * 512:(nb + 1) * 512], True, True,
                      reads=["WT", "U%d_%d" % (gl, nb)], writes=[pik])
            kb.copy("act", Sa[:, 0, nb * 512:(nb + 1) * 512], pr[:, :], reads=[prk], writes=[ka, "ubf0", "ubf1"])
            kb.copy("act", Sa[:, 1, nb * 512:(nb + 1) * 512], pi_[:, :], reads=[pik], writes=[ka, "ubf0", "ubf1"])
        src, dst, sk, dk = Sa, Sb, ka, kbk
        for l in range(10):
            d = 1 << l
            a_ = Pre[:, gp, l:l + 1]; b_ = Pim[:, gp, l:l + 1]; nb_ = nPim[:, gp, l:l + 1]
            kb.copy(eng, dst[:, :, 0:d], src[:, :, 0:d], reads=[sk], writes=[dk])
            kb.stt(eng, dst[:, 0, d:], src[:, 0, 0:NCH - d], a_, src[:, 0, d:], ALU.mult, ALU.add, reads=[sk] + pall, writes=[dk])
            kb.stt(eng, dst[:, 0, d:], src[:, 1, 0:NCH - d], nb_, dst[:, 0, d:], ALU.mult, ALU.add, reads=[sk] + pall, writes=[dk])
            kb.stt(eng, dst[:, 1, d:], src[:, 0, 0:NCH - d], b_, src[:, 1, d:], ALU.mult, ALU.add, reads=[sk] + pall, writes=[dk])
            kb.stt(eng, dst[:, 1, d:], src[:, 1, 0:NCH - d], a_, dst[:, 1, d:], ALU.mult, ALU.add, reads=[sk] + pall, writes=[dk])
            src, dst, sk, dk = dst, src, dk, sk
        kb.copy(eng, Sbf[:, gp, :, 1:NCH + 1], src[:, :, :], reads=[sk], writes=["Sbf%d" % gp])

    for gl in range(16):
        gp, par = gl // 2, gl % 2
        rows = slice(par * 64, par * 64 + 64)
        for nb in range(2):
            ps = pss[4 + (gl * 2 + nb) % 2]; pk = "ps%d" % (4 + (gl * 2 + nb) % 2)
            uk = "U%d_%d" % (gl, nb)
            kb.mm(ps[:, :], Mtab[:, gl, :], U[:, gl, nb * 512:(nb + 1) * 512], True, False, reads=["Mtab", uk], writes=[pk])
            kb.mm(ps[:, :], CLbr[rows, gp, :], Sbf[rows, gp, 0, nb * 512:(nb + 1) * 512], False, False,
                  reads=["CLbr", "Sbf%d" % gp, "Sbf0"], writes=[pk])
            kb.mm(ps[:, :], CLbni[rows, gp, :], Sbf[rows, gp, 1, nb * 512:(nb + 1) * 512], False, True,
                  reads=["CLbni", "Sbf%d" % gp, "Sbf0"], writes=[pk])
            kb.copy(kb.evac_eng(), U[:, gl, nb * 512:(nb + 1) * 512], ps[:, :], reads=[pk], writes=[uk])

    it = 0
    for ct in range(2):
        for qb in range(4):
            j = it % 2
            it += 1
            kb.dma("sp", u32[:, j, :], uv[:, ct, qb * 2048:(qb + 1) * 2048], writes=["u32_%d" % j])
            for t in range(8):
                ps = pss[6 + t % 2]; pk = "ps%d" % (6 + t % 2)
                for g8 in range(8):
                    gl = ct * 8 + g8
                    nb = qb // 2
                    kb.mm(ps[:, 0:256], esel[:, t, 112 - 16 * g8:240 - 16 * g8], U[:, gl, qb * 256:(qb + 1) * 256], g8 == 0, g8 == 7,
                          reads=["esel", "U%d_%d" % (gl, nb)], writes=[pk])
                z = gz[:, t % 2, :]; w_ = gw[:, t % 2, :]; s_ = gs_[:, t % 2, :]
                zk, wk, sk_ = "gz%d" % (t % 2), "gw%d" % (t % 2), "gs%d" % (t % 2)
                e2 = "pool" if t % 2 else "dve"
                kb.stt("dve", z, u32[:, j, t::8], dsk[:, ct:ct + 1], ps[:, 0:256], ALU.mult, ALU.add, reads=["u32_%d" % j, "dsk", pk], writes=[zk])
                kb.tt(e2, w_, z, z, ALU.mult, reads=[zk], writes=[wk])
                kb.ts(e2, w_, w_, 0.044715, ALU.mult, 1.0, ALU.add, reads=[wk], writes=[wk])
                kb.tt(e2, w_, w_, z, ALU.mult, reads=[wk, zk], writes=[wk])
                kb.act(s_, w_, AF.Sigmoid, scale=1.5957691216057308, reads=[wk], writes=[sk_])
                kb.tt(e2, yout[:, j, t::8], z, s_, ALU.mult, reads=[zk, sk_], writes=["yout%d" % j])
            kb.dma("act", y_d[ct * 128:(ct + 1) * 128, qb * 2048:(qb + 1) * 2048], yout[:, j, :], reads=["yout%d" % j])
    return kb


def run_k2(u_full, inp):
    esel, mask, ident = s5_consts()
    kb = build_k2()
    maps = []
    for core in range(NCORE):
        b, q = core // 4, core % 4
        gs = slice(q * 16, q * 16 + 16)

        def pair(a):
            a = np.asarray(a, np.float32)
            a = a.reshape((8, 2) + a.shape[1:])
            a = np.moveaxis(a, 0, 2)
            return np.ascontiguousarray(a.reshape((128, 8) + a.shape[3:]))

        lre = pair(inp["s5_lam_re"][0, gs]); lim = pair(inp["s5_lam_im"][0, gs])
        ldt = pair(np.repeat(inp["s5_log_dt"][0, gs][:, None], 64, axis=1))
        br = pair(inp["s5_b_re"][0, gs]); bi = pair(inp["s5_b_im"][0, gs])
        cr = pair(inp["s5_c_re"][0, gs].transpose(0, 2, 1)); ci = pair(inp["s5_c_im"][0, gs].transpose(0, 2, 1))
        dsk = vec_tiles(inp["s5_d"][0, q * 256:(q + 1) * 256])
        maps.append({"u": np.ascontiguousarray(u_full[b, :, q * 256:(q + 1) * 256].T), "lre": lre, "lim": lim, "ldt": ldt,
                     "br": br, "bi": bi, "cr": cr, "ci": ci, "dsk": dsk, "esel": esel, "mask": mask, "ident": ident})
    res = run_spmd(kb, maps)
    y = np.empty((B, S, 1024), np.float32)
    for core in range(NCORE):
        b, q = core // 4, core % 4
        y[b, :, q * 256:(q + 1) * 256] = res[core]["y"].T
    return y


NT = TPC + 2
BLK5 = [(0, 2)] + [(2 + 512 * i, 2 + 512 * (i + 1)) for i in range(4)]


def tok_shard_T_halo(a):
    out = []
    for b in range(B):
        for q in range(4):
            s0 = q * TPC
            blk = np.zeros((a.shape[2], NT), np.float32)
            blk[:, 2:] = a[b, s0:s0 + TPC, :].T
            if s0 > 0:
                blk[:, 0:2] = a[b, s0 - 2:s0, :].T
            out.append(blk)
    return out


def build_mixout(glu):
    kb = KB()
    a_d = kb.din("a", [1024, NT]); r_d = kb.din("res", [1024, NT])
    gate_d = kb.din("gate", [128, 8])
    wo_d = kb.din("w_out", [1024, 1024])
    if glu:
        wg_d = kb.din("w_glu", [1024, 1024])
    o_d = kb.dout("out", [1024, NT])
    wo = kb.sb("wo", [128, 8, 1024], BF16)
    wg = kb.sb("wg", [128, 8, 1024], BF16) if glu else None
    gate = kb.sb("gate", [128, 8])
    a32 = kb.sb("a32", [128, 2, 8, 512]); abf = kb.sb("abf", [128, 2, 8, 512], BF16); gl = kb.sb("gl", [128, 2, 8, 512], BF16)
    r32 = kb.sb("r32", [128, 2, 8, 512]); sig = kb.sb("sig", [128, 2, 512])
    pss = [kb.ps("p%d" % i) for i in range(8)]
    kb.dma("sp", gate[:], gate_d, writes=["gate"])
    wov = wo_d.rearrange("(kt p) m -> p kt m", p=128)
    for kt in range(8):
        kb.dma("pool", wo[:, kt, :], wov[:, kt, :], writes=["wo"])
    if glu:
        wgv = wg_d.rearrange("(kt p) m -> p kt m", p=128)
        for kt in range(8):
            kb.dma("pool", wg[:, kt, :], wgv[:, kt, :], writes=["wg"])
    av = a_d.rearrange("(kt p) t -> p kt t", p=128)
    rv = r_d.rearrange("(kt p) t -> p kt t", p=128)
    ov = o_d.rearrange("(kt p) t -> p kt t", p=128)
    pi = 0
    for bi, (c0, c1) in enumerate(BLK5):
        n = c1 - c0
        j = bi % 2
        kb.dma("sp", r32[:, j, :, 0:n], rv[:, :, c0:c1], writes=["r32_%d" % j])
        kb.dma("pool", abf[:, j, :, 0:n], av[:, :, c0:c1], writes=["abf%d" % j])
        if glu:
            kb.dma("act", a32[:, j, :, 0:n], av[:, :, c0:c1], writes=["a32_%d" % j])
            for mt in range(8):
                ps = pss[pi % 8]; pk = "p%d" % (pi % 8); pi += 1
                for kt in range(8):
                    kb.mm(ps[:, 0:n], wg[:, kt, mt * 128:(mt + 1) * 128], abf[:, j, kt, 0:n], kt == 0, kt == 7, reads=["wg", "abf%d" % j], writes=[pk])
                sk = "sig%d" % (mt % 2)
                kb.act(sig[:, mt % 2, 0:n], ps[:, 0:n], AF.Sigmoid, reads=[pk], writes=[sk])
                kb.tt("dve", gl[:, j, mt, 0:n], a32[:, j, mt, 0:n], sig[:, mt % 2, 0:n], ALU.mult, reads=[sk, "a32_%d" % j], writes=["gl%d_%d" % (j, mt)])
            src, skey = gl, (lambda kt: "gl%d_%d" % (j, kt))
        else:
            src, skey = abf, (lambda kt: "abf%d" % j)
        for mt in range(8):
            ps = pss[pi % 8]; pk = "p%d" % (pi % 8); pi += 1
            for kt in range(8):
                kb.mm(ps[:, 0:n], wo[:, kt, mt * 128:(mt + 1) * 128], src[:, j, kt, 0:n], kt == 0, kt == 7, reads=["wo", skey(kt)], writes=[pk])
            kb.stt("dve", r32[:, j, mt, 0:n], ps[:, 0:n], gate[:, mt:mt + 1], r32[:, j, mt, 0:n], ALU.mult, ALU.add,
                   reads=[pk, "gate", "r32_%d" % j], writes=["r32_%d" % j])
        kb.dma("sp", ov[:, :, c0:c1], r32[:, j, :, 0:n], reads=["r32_%d" % j])
    return kb


def run_mixout(a_list, res_list, gates, w_out, w_glu=None):
    kb = build_mixout(w_glu is not None)
    maps = []
    for core in range(NCORE):
        m = {"a": a_list[core], "res": res_list[core], "gate": gates[core // 4], "w_out": np.ascontiguousarray(w_out)}
        if w_glu is not None:
            m["w_glu"] = np.ascontiguousarray(w_glu)
        maps.append(m)
    res = run_spmd(kb, maps)
    return [r["out"] for r in res]


NFT = DFF // 128


def emit_norm_blocks(kb, h, hkey, gs, shift, hn, hnkey, ones_bf, blocks, scr, nps, out_off=0, gskey="gs_n"):
    sq, rstd, tmp = scr["sq"], scr["rstd"], scr["tmp"]
    for bi, (c0, c1) in enumerate(blocks):
        n = c1 - c0
        ps = nps[bi % len(nps)]; psk = "nps%d" % (bi % len(nps))
        for kt in range(8):
            sqk = "nsq%d" % (kt % 2)
            kb.act(sq[:, kt % 2, 0:n], h[:, kt, c0:c1], AF.Square, reads=[hkey(kt, bi)], writes=[sqk])
            kb.mm(ps[:, 0:n], ones_bf[:, :], sq[:, kt % 2, 0:n], kt == 0, kt == 7, reads=[sqk, "ones"], writes=[psk])
        kb.act(rstd[:, 0:n], ps[:, 0:n], AF.Ln, bias=scr["eps"][:, 0:1], reads=[psk, "neps"], writes=["nrstd"])
        kb.act(rstd[:, 0:n], rstd[:, 0:n], AF.Exp, scale=-0.5, reads=["nrstd"], writes=["nrstd"])
        for kt in range(8):
            o = hn[:, kt, c0 - out_off:c1 - out_off]
            if shift is None:
                kb.stt("dve", o, h[:, kt, c0:c1], gs[:, kt:kt + 1], rstd[:, 0:n], ALU.mult, ALU.mult,
                       reads=[hkey(kt, bi), "nrstd", gskey], writes=[hnkey(kt, bi)])
            else:
                tk = "ntmp%d" % (kt % 2)
                kb.stt("dve", tmp[:, kt % 2, 0:n], h[:, kt, c0:c1], gs[:, kt:kt + 1], rstd[:, 0:n], ALU.mult, ALU.mult,
                       reads=[hkey(kt, bi), "nrstd", gskey], writes=[tk])
                kb.act(o, tmp[:, kt % 2, 0:n], AF.Identity, bias=shift[:, kt:kt + 1], reads=[tk, "shift_n"], writes=[hnkey(kt, bi)])


def build_ffn(final):
    kb = KB()
    h_d = kb.din("hT", [1024, NT])
    g_d = kb.din("g", [128, 8]); sh_d = kb.din("shift", [128, 8]); sc_d = kb.din("scale", [128, 8]); gate_d = kb.din("gate", [128, 8])
    wu_d = kb.din("w_up", [1024, 2 * DFF]); wd_d = kb.din("w_down", [DFF, 1024])
    cw_d = kb.din("cw", [128, 3, NFT]); cb_d = kb.din("cb", [128, NFT]); hm_d = kb.din("hmask", [128, 1])
    if final:
        fg_d = kb.din("fg", [128, 8])
    o_d = kb.dout("out", [1024, TPC])
    HC = 1026
    h32 = kb.sb("h32", [128, 8, HC]); hn = kb.sb("hn", [128, 8, HC], BF16)
    g_t = kb.sb("g_t", [128, 8]); sh_t = kb.sb("sh_t", [128, 8]); sc_t = kb.sb("sc_t", [128, 8]); gs_t = kb.sb("gs_t", [128, 8]); gate = kb.sb("gate", [128, 8])
    cw = kb.sb("cw", [128, 3, NFT]); cb = kb.sb("cb", [128, NFT]); hm = kb.sb("hm", [128, 1])
    ones = kb.sb("ones", [128, 128], BF16)
    wd = kb.sb("wd", [128, NFT, 1024], BF16)
    wa = kb.sb("wa", [128, 2, 8, 256], BF16); wb = kb.sb("wb", [128, 2, 8, 256], BF16)
    G = kb.sb("G", [128, NFT, 1024], BF16)
    a32 = kb.sb("a32", [128, 2, 1026]); asave = kb.sb("asave", [128, NFT, 2])
    ac = kb.sb("ac", [128, 2, 512]); sg = kb.sb("sg", [128, 2, 512])
    scr = norm_scratch(kb)
    if final:
        fg = kb.sb("fg", [128, 8]); h2 = kb.sb("h2", [128, 8, 512])
    nps = [kb.ps("nps%d" % i) for i in range(2)]
    aps = [kb.ps("aps%d" % i) for i in range(2)]
    bps = [kb.ps("bps%d" % i) for i in range(2)]
    dps = [kb.ps("dps%d" % i) for i in range(2)]
    kb.memset("pool", ones[:], 1.0 / D, writes=["ones"])
    for t_, d_, k_ in ((g_t, g_d, "g_n"), (sh_t, sh_d, "shift_n"), (sc_t, sc_d, "scale_n"), (gate, gate_d, "gate"), (cw, cw_d, "cw"), (cb, cb_d, "cb"), (hm, hm_d, "hm")):
        kb.dma("sp", t_[:], d_, writes=[k_])
    if final:
        kb.dma("sp", fg[:], fg_d, writes=["fg"])
    emit_gs(kb, "n", g_t, sc_t, gs_t)
    wdv = wd_d.rearrange("(ft p) m -> p ft m", p=128)
    for ft in range(NFT):
        kb.dma("pool", wd[:, ft, :], wdv[:, ft, :], writes=["wd"])
    hv = h_d.rearrange("(kt p) t -> p kt t", p=128)
    ov = o_d.rearrange("(kt p) t -> p kt t", p=128)
    wuv = wu_d.rearrange("(kt p) m -> p kt m", p=128)
    wi = 0
    for hf in range(2):
        if hf == 0:
            base, ncol = 0, 1026
            blocks = [(0, 2), (2, 514), (514, 1026)]
        else:
            base, ncol = 1026, 1024
            blocks = [(0, 512), (512, 1024)]
        for kt in range(8):
            kb.dma(("sp", "act")[kt % 2], h32[:, kt, 0:ncol], hv[:, kt, base:base + ncol], writes=["h32_%d" % kt])
        emit_norm_blocks(kb, h32, lambda kt, bi: "h32_%d" % kt, gs_t, sh_t, hn, lambda kt, bi: "hn%d" % kt, ones, blocks, scr, nps)
        own = blocks[1:] if hf == 0 else blocks
        ooff = 2 if hf == 0 else 0
        for fg_ in range(NFT // 2):
            j = wi % 2; wi += 1
            for kt in range(8):
                kb.dma("pool", wa[:, j, kt, :], wuv[:, kt, fg_ * 256:(fg_ + 1) * 256], writes=["wa%d" % j])
                kb.dma("pool", wb[:, j, kt, :], wuv[:, kt, DFF + fg_ * 256:DFF + (fg_ + 1) * 256], writes=["wb%d" % j])
            for jj in range(2):
                ft = 2 * fg_ + jj
                ab = ft % 2
                ak = "a32_%d" % ab
                if hf == 0:
                    ps = aps[0]; pk = "aps0"
                    for kt in range(8):
                        kb.mm(ps[:, 0:2], wa[:, j, kt, jj * 128:(jj + 1) * 128], hn[:, kt, 0:2], kt == 0, kt == 7, reads=["wa%d" % j, "hn%d" % kt], writes=[pk])
                    kb.ts("dve", a32[:, ab, 0:2], ps[:, 0:2], hm[:, 0:1], ALU.mult, reads=[pk, "hm"], writes=[ak])
                else:
                    kb.copy("dve", a32[:, ab, 0:2], asave[:, ft, :], reads=["asave"], writes=[ak])
                for bi, (c0, c1) in enumerate(own):
                    pa = aps[bi % 2]; pak = "aps%d" % (bi % 2)
                    pb = bps[bi % 2]; pbk = "bps%d" % (bi % 2)
                    for kt in range(8):
                        kb.mm(pa[:, :], wa[:, j, kt, jj * 128:(jj + 1) * 128], hn[:, kt, c0:c1], kt == 0, kt == 7, reads=["wa%d" % j, "hn%d" % kt], writes=[pak])
                    for kt in range(8):
                        kb.mm(pb[:, :], wb[:, j, kt, jj * 128:(jj + 1) * 128], hn[:, kt, c0:c1], kt == 0, kt == 7, reads=["wb%d" % j, "hn%d" % kt], writes=[pbk])
                    o0 = c0 - ooff
                    kb.copy("act", a32[:, ab, 2 + o0:2 + o0 + 512], pa[:, :], reads=[pak], writes=[ak])
                    x = bi % 2
                    ack, sgk = "ac%d" % x, "sg%d" % x
                    kb.ts("dve", ac[:, x, :], a32[:, ab, 2 + o0:2 + o0 + 512], cw[:, 2, ft:ft + 1], ALU.mult, cb[:, ft:ft + 1], ALU.add, reads=[ak, "cw", "cb"], writes=[ack])
                    kb.stt("dve", ac[:, x, :], a32[:, ab, 1 + o0:1 + o0 + 512], cw[:, 1, ft:ft + 1], ac[:, x, :], ALU.mult, ALU.add, reads=[ak, "cw", ack], writes=[ack])
                    kb.stt("dve", ac[:, x, :], a32[:, ab, o0:o0 + 512], cw[:, 0, ft:ft + 1], ac[:, x, :], ALU.mult, ALU.add, reads=[ak, "cw", ack], writes=[ack])
                    kb.act(sg[:, x, :], ac[:, x, :], AF.Silu, reads=[ack], writes=[sgk])
                    kb.tt("dve", G[:, ft, o0:o0 + 512], sg[:, x, :], pb[:, :], ALU.mult, reads=[sgk, pbk], writes=["G%d_%d" % (ft, bi)])
                if hf == 0:
                    kb.copy("pool", asave[:, ft, :], a32[:, ab, 1024:1026], reads=[ak], writes=["asave"])
        for bi, (c0, c1) in enumerate(own):
            o0 = c0 - ooff
            for mt in range(8):
                ps = dps[mt % 2]; pk = "dps%d" % (mt % 2)
                for ft in range(NFT):
                    kb.mm(ps[:, :], wd[:, ft, mt * 128:(mt + 1) * 128], G[:, ft, o0:o0 + 512], ft == 0, ft == NFT - 1, reads=["wd", "G%d_%d" % (ft, bi)], writes=[pk])
                if final:
                    kb.stt("dve", h2[:, mt, :], ps[:, :], gate[:, mt:mt + 1], h32[:, mt, c0:c1], ALU.mult, ALU.add, reads=[pk, "gate", "h32_%d" % mt], writes=["h2_%d" % mt])
                else:
                    kb.stt("dve", h32[:, mt, c0:c1], ps[:, :], gate[:, mt:mt + 1], h32[:, mt, c0:c1], ALU.mult, ALU.add, reads=[pk, "gate", "h32_%d" % mt], writes=["h32_%d" % mt])
            tok0 = hf * 1024 + o0
            if final:
                emit_norm_blocks(kb, h2, lambda kt, b_: "h2_%d" % kt, fg, None, h2, lambda kt, b_: "h2_%d" % kt, ones, [(0, 512)], scr, nps, gskey="fg")
                kb.dma("sp", ov[:, :, tok0:tok0 + 512], h2[:, :, :], reads=["h2_%d" % kt for kt in range(8)])
            else:
                kb.dma("sp", ov[:, :, tok0:tok0 + 512], h32[:, :, c0:c1], reads=["h32_%d" % kt for kt in range(8)])
    return kb


def run_ffn(h_list, inp, mods, layer, final):
    kb = build_ffn(final)
    l = layer * 2 + 1
    cw = np.ascontiguousarray(inp["ffn_conv_w"][layer].reshape(3, NFT, 128).transpose(2, 0, 1))
    cb = vec_tiles(inp["ffn_conv_b"][layer])
    maps = []
    for core in range(NCORE):
        sh, sc, gt = mod_tiles(mods, l, core // 4)
        m = {"hT": h_list[core], "g": vec_tiles(inp["norm_g"][layer, 1]), "shift": sh, "scale": sc, "gate": gt,
             "w_up": np.ascontiguousarray(inp["ffn_w_up"][layer]), "w_down": np.ascontiguousarray(inp["ffn_w_down"][layer]),
             "cw": cw, "cb": cb, "hmask": np.full((128, 1), 0.0 if core % 4 == 0 else 1.0, np.float32)}
        if final:
            m["fg"] = vec_tiles(inp["final_g"])
        maps.append(m)
    res = run_spmd(kb, maps)
    return [r["out"] for r in res]


MASKV = -240000.0


def attn_consts():
    mneg = np.zeros((128, 128), np.float32)
    for s in range(128):
        mneg[s, :s] = MASKV
    return mneg, np.eye(128, dtype=np.float32)


def build_k5():
    kb = KB()
    q_d = kb.din("q", [256, S]); k_d = kb.din("k", [256, S]); v_d = kb.din("v", [4, 128, 64, 64])
    fl_d = kb.din("fl", [128, 256]); bf_d = kb.din("bf", [128, 1]); tt_d = kb.din("ttri", [128, 128])
    mneg_d = kb.din("mneg", [128, 128]); id_d = kb.din("ident", [128, 128])
    o_d = kb.dout("o", [256, S])
    fs_scr = kb.nc.dram_tensor("fs_scr", [4, 32, 3, 256], BF16).ap()

    qt = kb.sb("qt", [128, 2, S], BF16); kt_ = kb.sb("kt", [128, 2, S], BF16)
    va = kb.sb("va", [128, 2, 64, 65], BF16); vst = kb.sb("vst", [128, 64, 64], BF16)
    pT = kb.sb("pT", [128, 3, 512], BF16)
    mneg = kb.sb("mneg", [128, 128], BF16); identb = kb.sb("identb", [128, 128], BF16); ident = kb.sb("ident", [128, 128])
    fl = kb.sb("fl", [128, 256]); F = kb.sb("F", [128, 256]); onesr = kb.sb("onesr", [128, 256]); ft = kb.sb("ftmp", [128, 256])
    Fs = kb.sb("Fs", [128, 3, 256], BF16)
    ttri = kb.sb("ttri", [128, 128]); foff = kb.sb("foff", [128, 1])
    nbf = kb.sb("nbf", [128, 1]); one1 = kb.sb("one1", [128, 1])
    nF = kb.sb("nF", [128, 4, 64])
    ones65 = kb.sb("ones65", [128, 64])
    rden = kb.sb("rden", [128, 512]); bcs = kb.sb("bcs", [128, 512]); osb = kb.sb("osb", [128, 2, 512])
    sps = [kb.ps("sps%d" % i) for i in range(3)]
    ops = [kb.ps("ops%d" % i) for i in range(2)]
    bps = kb.ps("bcps")
    tps = kb.ps("tps")

    kb.dma("sp", fl[:], fl_d, writes=["fl"]); kb.dma("sp", nbf[:], bf_d, writes=["nbf"])
    kb.dma("pool", mneg[:], mneg_d, writes=["mneg"]); kb.dma("pool", identb[:], id_d, writes=["identb"]); kb.dma("sp", ident[:], id_d, writes=["ident"])
    kb.memset("pool", onesr[:], 1.0, writes=["onesr"]); kb.memset("pool", one1[:], 1.0, writes=["one1"])
    kb.memset("pool", ones65[:], 1.0, writes=["ones65"])
    kb.memset("pool", va[:, :, :, 64:65], 1.0, writes=["va0", "va1"])
    kb.memset("pool", kt_[64:67, :, :], 1.0, writes=["kt0", "kt1"])
    kb.dma("sp", ttri[:], tt_d, writes=["ttri"])
    kb.ts("dve", nbf[:], nbf[:], -1.0, ALU.mult, reads=["nbf"], writes=["nbf"])
    kb.act(ft[:], fl[:], AF.Exp, bias=nbf[:, 0:1], scale=-1.0, reads=["fl", "nbf"], writes=["ft"])
    kb.act(ft[:], ft[:], AF.Ln, bias=one1[:, 0:1], reads=["ft", "one1"], writes=["ft"])
    kb.ts("dve", ft[:], ft[:], -1.0, ALU.mult, reads=["ft"], writes=["ft"])
    kb.P.op("dve", lambda e: e.tensor_tensor_scan(out=F[:], data0=onesr[:], data1=ft[:], initial=0.0, op0=ALU.mult, op1=ALU.add),
            ["onesr", "ft"], ["F"])
    kb.mm(tps[:, 0:1], ttri[:, :], F[:, 255:256], True, True, reads=["ttri", "F"], writes=["tps"])
    kb.copy("dve", foff[:], tps[:, 0:1], reads=["tps"], writes=["foff"])
    kb.ts("dve", F[:], F[:], foff[:, 0:1], ALU.add, reads=["F", "foff"], writes=["F"])
    for half in range(2):
        kb.tr(tps[:, 128 + half * 128:256 + half * 128], F[:, half * 128:(half + 1) * 128], ident[:, :], reads=["F", "ident"], writes=["tps"])
    kb.ts("dve", nF[:, :, :].rearrange("p h (g f) -> p f h g", f=2), tps[:, 128:384].rearrange("p (f h g) -> p f h g", f=2, h=4), -1.0, ALU.mult,
          reads=["tps"], writes=["nF"])
    kb.ts("dve", ft[:], F[:], 8.0, ALU.mult, reads=["F", "ft"], writes=["ft"])
    kb.copy("dve", Fs[:, 0, :], ft[:], reads=["ft"], writes=["Fs0"])
    kb.tt("dve", ft[:], ft[:], Fs[:, 0, :], ALU.subtract, reads=["ft", "Fs0"], writes=["ft"])
    kb.copy("dve", Fs[:, 1, :], ft[:], reads=["ft"], writes=["Fs1"])
    kb.tt("dve", ft[:], ft[:], Fs[:, 1, :], ALU.subtract, reads=["ft", "Fs1"], writes=["ft"])
    kb.copy("dve", Fs[:, 2, :], ft[:], reads=["ft"], writes=["Fs2"])
    kb.dma("sp", fs_scr.rearrange("h g k t -> (h g) k t"), Fs[:, :, :], reads=["Fs0", "Fs1", "Fs2"], writes=["fs_scr"])

    for h in range(4):
        sl = h % 2
        qk, kk, vk = "qt%d" % sl, "kt%d" % sl, "va%d" % sl
        for hf in range(2):
            cs = slice(hf * 4096, (hf + 1) * 4096)
            kb.dma("pool", qt[0:64, sl, cs], q_d[h * 64:(h + 1) * 64, cs], writes=[qk])
            kb.dma("pool", kt_[0:64, sl, cs], k_d[h * 64:(h + 1) * 64, cs], writes=[kk])
        kb.dma("sp", qt[64:67, sl, :].rearrange("k (g t) -> k g t", t=256), fs_scr[h].rearrange("g k t -> k g t"), reads=["fs_scr"], writes=[qk])
        kb.dma("pool", vst[:, :, :], v_d[h], writes=["vst"])
        kb.copy("pool", va[:, sl, :, 0:64], vst[:, :, :], reads=["vst"], writes=[vk])
        work = []
        for I in range(16):
            for j in range(4 * I + 4):
                work.append((I, j))
        pend = None
        for wi, (I, j) in enumerate(work):
            nblk = 4 * I + 4
            m = j - 4 * I
            c0 = 0 if m < 0 else 128 * m
            sp = sps[wi % 3]; spk = "sps%d" % (wi % 3)
            kb.mm(sp[:, c0:512], kt_[0:67, sl, j * 128:(j + 1) * 128], qt[0:67, sl, I * 512 + c0:(I + 1) * 512], True, m < 0,
                  reads=[kk, qk], writes=[spk])
            if m >= 0:
                kb.mm(sp[:, c0:c0 + 128], identb[:, :], mneg[:, :], False, True, reads=["identb", "mneg"], writes=[spk])
            ptk = "pT%d" % (wi % 3)
            kb.act(pT[:, wi % 3, c0:512], sp[:, c0:512], AF.Exp, bias=nF[:, h, j:j + 1], scale=0.125, reads=[spk, "nF"], writes=[ptk])
            if pend is not None:
                pend()
            op = ops[I % 2]; opk = "ops%d" % (I % 2)

            def pv(op=op, opk=opk, j=j, c0=c0, nblk=nblk, ptk=ptk, wi=wi, I=I):
                kb.mm(op[0:65, c0:512], va[:, sl, j, 0:65], pT[:, wi % 3, c0:512], j == 0, j == nblk - 1, reads=[vk, ptk], writes=[opk])
                if j == nblk - 1:
                    x = I % 2
                    kb.P.op("dve", lambda e: e.reciprocal(out=rden[64:65, :], in_=op[64:65, :]), [opk], ["rden"])
                    kb.mm(bps[0:64, :], ones65[64:65, 0:64], rden[64:65, :], True, True, reads=["ones65", "rden"], writes=["bps"])
                    kb.copy("dve", bcs[0:64, :], bps[0:64, :], reads=["bps"], writes=["bcs"])
                    kb.tt("dve", osb[0:64, x, :], op[0:64, :], bcs[0:64, :], ALU.mult, reads=[opk, "bcs"], writes=["osb%d" % x])
                    kb.dma("sp", o_d[h * 64:(h + 1) * 64, I * 512:(I + 1) * 512], osb[0:64, x, :], reads=["osb%d" % x])
            pend = pv
        pend()
    return kb


def run_k5(proj_list):
    mneg, ident = attn_consts()
    ttri = np.zeros((128, 128), np.float32)
    for h_ in range(4):
        for s1 in range(32):
            ttri[h_ * 32 + s1, h_ * 32 + s1 + 1:h_ * 32 + 32] = 1.0
    full = [np.concatenate([proj_list[b * 4 + q] for q in range(4)], axis=1) for b in range(B)]
    kb = build_k5()
    maps = []
    for core in range(NCORE):
        b, hq = core // 4, core % 4
        P_ = full[b]
        rows = slice(hq * 256, hq * 256 + 256)
        v = P_[2048 + hq * 256:2048 + hq * 256 + 256, :]
        v = v.reshape(4, 64, 64, 128).transpose(0, 3, 2, 1)
        maps.append({"q": np.ascontiguousarray(P_[rows]), "k": np.ascontiguousarray(P_[1024 + hq * 256:1024 + hq * 256 + 256]),
                     "v": np.ascontiguousarray(v), "fl": np.ascontiguousarray(P_[3072 + hq * 4:3072 + hq * 4 + 4].reshape(128, 256)),
                     "mneg": mneg, "ident": ident, "bf": None, "ttri": ttri})
    return kb, maps


def run_k5_full(proj_list, b_f):
    kb, maps = run_k5(proj_list)
    for core in range(NCORE):
        hq = core % 4
        maps[core]["bf"] = np.ascontiguousarray(np.repeat(b_f[hq * 4:hq * 4 + 4].astype(np.float32), 32).reshape(128, 1))
    res = run_spmd(kb, maps)
    o = np.empty((B, S, 1024), np.float32)
    for core in range(NCORE):
        b, hq = core // 4, core % 4
        o[b, :, hq * 256:(hq + 1) * 256] = res[core]["o"].T
    return o


def add_halo(lst):
    out = []
    for core in range(NCORE):
        a = lst[core]
        blk = np.zeros((a.shape[0], NT), np.float32)
        blk[:, 2:] = a
        if core % 4 != 0:
            blk[:, 0:2] = lst[core - 1][:, TPC - 2:TPC]
        out.append(blk)
    return out


def kernel(x, c, norm_g, ada_w, ada_b, s5_w_in, s5_lam_re, s5_lam_im, s5_log_dt, s5_b_re, s5_b_im,
           s5_c_re, s5_c_im, s5_d, s5_w_glu, s5_w_out, fox_w_in, fox_b_f, fox_w_out,
           ffn_w_up, ffn_conv_w, ffn_conv_b, ffn_w_down, final_g):
    f = lambda a: np.asarray(a, np.float32)
    inp = {"s5_lam_re": f(s5_lam_re), "s5_lam_im": f(s5_lam_im), "s5_log_dt": f(s5_log_dt), "s5_b_re": f(s5_b_re),
           "s5_b_im": f(s5_b_im), "s5_c_re": f(s5_c_re), "s5_c_im": f(s5_c_im), "s5_d": f(s5_d),
           "norm_g": f(norm_g), "ffn_w_up": f(ffn_w_up), "ffn_conv_w": f(ffn_conv_w), "ffn_conv_b": f(ffn_conv_b),
           "ffn_w_down": f(ffn_w_down), "final_g": f(final_g)}
    x = f(x)
    mods = run_k0(f(c), f(ada_w), f(ada_b))
    u = run_normproj(tok_shard_T(x), inp["norm_g"][0, 0], mods, 0, f(s5_w_in)[0])
    y = run_k2(tok_unshard_T(u, 1024), inp)
    gates0 = [mod_tiles(mods, 0, b)[2] for b in range(B)]
    h1 = run_mixout(tok_shard_T_halo(y), tok_shard_T_halo(x), gates0, f(s5_w_out)[0], f(s5_w_glu)[0])
    h2 = run_ffn(h1, inp, mods, 0, False)
    proj = run_normproj(h2, inp["norm_g"][1, 0], mods, 2, f(fox_w_in)[0])
    o = run_k5_full(proj, f(fox_b_f)[0])
    gates2 = [mod_tiles(mods, 2, b)[2] for b in range(B)]
    h3 = run_mixout(tok_shard_T_halo(o), add_halo(h2), gates2, f(fox_w_out)[0])
    out = run_ffn(h3, inp, mods, 1, True)
    return tok_unshard_T(out, 1024)
```
